# Optimizing a Trainium2 kernel written in Bass

```python
import math
import jax, jax.numpy as jnp
from jax import lax
import numpy as np

D_MODEL = 4096
BATCH = 1
SEQ = 8192
DEPTH = 1

POOL_WINDOWS = (2, 4, 8, 16)
POOL_WIDTH = D_MODEL
POOL_GROUP = POOL_WIDTH // len(POOL_WINDOWS)
DN_HEADS = 32
DN_HEAD_DIM = D_MODEL // DN_HEADS
DN_KEY = DN_HEADS * DN_HEAD_DIM
DN_VAL = DN_HEADS * DN_HEAD_DIM
CONV_K = 4
CONV_CH = 2 * DN_KEY + DN_VAL
CHUNK = 64
IN_SPLITS = (POOL_WIDTH, POOL_WIDTH, DN_KEY, DN_KEY, DN_VAL, DN_VAL, DN_HEADS, DN_HEADS, D_MODEL, D_MODEL)
IN_COLS = 4 * D_MODEL + 2 * DN_KEY + 2 * DN_VAL + 2 * DN_HEADS - 2 * D_MODEL + 2 * POOL_WIDTH - 2 * D_MODEL + 0 if False else (2 * POOL_WIDTH + 2 * DN_KEY + 2 * DN_VAL + 2 * DN_HEADS + 2 * D_MODEL)
NORM_EPS = 1e-6
L2_EPS = 1e-6

kernel_name = "hybrid_pool_gated_deltanet_parallel"


def rmsnorm(x, gain):
    xf = x.astype(jnp.float32)
    y = xf * lax.rsqrt(jnp.mean(xf * xf, axis=-1, keepdims=True) + NORM_EPS)
    return (y * gain.astype(jnp.float32)).astype(x.dtype)


def l2norm(x):
    return x * lax.rsqrt(jnp.sum(x * x, axis=-1, keepdims=True) + L2_EPS)


def causal_mean_minus_self(u, window):
    b, s, c = u.shape
    uf = u.astype(jnp.float32)
    csum = jnp.pad(jnp.cumsum(uf, axis=1), ((0, 0), (1, 0), (0, 0)))
    upper = csum[:, 1:]
    lower = jnp.concatenate([jnp.zeros((b, window - 1, c), jnp.float32), csum[:, : s - window + 1]], axis=1)
    count = jnp.minimum(jnp.arange(1, s + 1, dtype=jnp.float32), float(window))
    mean = (upper - lower) / count[None, :, None]
    return (mean - uf).astype(u.dtype)


def causal_depthwise_conv(u, w):
    c = u.shape[-1]
    return lax.conv_general_dilated(
        u, w[:, None, :].astype(u.dtype), window_strides=(1,), padding=[(CONV_K - 1, 0)],
        dimension_numbers=("NWC", "WIO", "NWC"), feature_group_count=c)


def gated_delta_rule_chunked(q, k, v, g, beta):
    b, s, h, dk = q.shape
    dv = v.shape[-1]
    n = s // CHUNK
    to_chunks = lambda t: jnp.moveaxis(t.reshape(b, n, CHUNK, h, -1), 3, 1)
    qc, kc, vc = to_chunks(q), to_chunks(k), to_chunks(v)
    gc = jnp.cumsum(jnp.moveaxis(g.reshape(b, n, CHUNK, h), 3, 1), axis=-1)
    bc = jnp.moveaxis(beta.reshape(b, n, CHUNK, h), 3, 1)

    causal = jnp.tril(jnp.ones((CHUNK, CHUNK), dtype=bool))
    strict = jnp.tril(jnp.ones((CHUNK, CHUNK), dtype=bool), k=-1)
    diff = gc[..., :, None] - gc[..., None, :]
    decay_mat = jnp.where(causal, jnp.exp(jnp.where(causal, diff, 0.0)), 0.0)

    k_beta = kc * bc[..., None]
    a_mat = jnp.where(strict, jnp.einsum("bhnid,bhnjd->bhnij", k_beta, kc) * decay_mat, 0.0)
    m_mat = a_mat + jnp.eye(CHUNK, dtype=jnp.float32)
    rhs = jnp.concatenate([vc * bc[..., None], k_beta * jnp.exp(gc)[..., None]], axis=-1)
    sol = lax.linalg.triangular_solve(m_mat, rhs, left_side=True, lower=True, unit_diagonal=True)
    u_c, w_c = sol[..., :dv], sol[..., dv:]
    qk = jnp.where(causal, jnp.einsum("bhnid,bhnjd->bhnij", qc, kc) * decay_mat, 0.0)

    def step(state, inp):
        q_i, k_i, u_i, w_i, qk_i, g_i = inp
        v_new = u_i - jnp.einsum("bhck,bhkv->bhcv", w_i, state)
        o_i = jnp.einsum("bhck,bhkv->bhcv", q_i * jnp.exp(g_i)[..., None], state) + jnp.einsum("bhij,bhjv->bhiv", qk_i, v_new)
        g_last = g_i[..., -1]
        k_dec = k_i * jnp.exp(g_last[..., None] - g_i)[..., None]
        state = state * jnp.exp(g_last)[..., None, None] + jnp.einsum("bhck,bhcv->bhkv", k_dec, v_new)
        return state, o_i

    xs = tuple(jnp.moveaxis(t, 2, 0) for t in (qc, kc, u_c, w_c, qk, gc))
    state0 = jnp.zeros((b, h, dk, dv), jnp.float32)
    _, o = lax.scan(step, state0, xs)
    return jnp.transpose(o, (1, 0, 3, 2, 4)).reshape(b, s, h, dv)


def setup_inputs(seed: int = 0) -> dict:
    key = jax.random.key(seed)
    ks = jax.random.split(key, 14)
    f = jnp.float32
    nrm = lambda k, shape, scale: jax.random.normal(k, shape, f) * scale
    x = jax.random.normal(ks[0], (BATCH, SEQ, D_MODEL), f)
    norm_gain = 1.0 + nrm(ks[1], (DEPTH, D_MODEL), 0.02)
    w_in = nrm(ks[2], (DEPTH, D_MODEL, IN_COLS), D_MODEL ** -0.5)
    w_qkv_conv = nrm(ks[3], (DEPTH, CONV_K, CONV_CH), CONV_K ** -0.5)
    pool_mix = nrm(ks[4], (DEPTH, len(POOL_WINDOWS), POOL_GROUP, POOL_GROUP), POOL_GROUP ** -0.5)
    pool_scale = 1.0 + nrm(ks[5], (DEPTH, POOL_WIDTH), 0.1)
    w_pool_proj = nrm(ks[6], (DEPTH, POOL_WIDTH, D_MODEL), POOL_WIDTH ** -0.5)
    a_log = jnp.log(jax.random.uniform(ks[7], (DEPTH, DN_HEADS), f, minval=1.0, maxval=16.0))
    dt = jnp.exp(jax.random.uniform(ks[8], (DEPTH, DN_HEADS), f, minval=math.log(1e-3), maxval=math.log(1e-1)))
    dt_bias = dt + jnp.log(-jnp.expm1(-dt))
    dn_head_norm = 1.0 + nrm(ks[9], (DEPTH, DN_HEAD_DIM), 0.02)
    w_dn_proj = nrm(ks[10], (DEPTH, DN_VAL, D_MODEL), DN_VAL ** -0.5)
    w_out = nrm(ks[11], (DEPTH, D_MODEL, D_MODEL), D_MODEL ** -0.5)
    final_norm_gain = 1.0 + nrm(ks[12], (D_MODEL,), 0.02)
    return {"x": x, "norm_gain": norm_gain, "w_in": w_in, "w_qkv_conv": w_qkv_conv,
            "pool_mix": pool_mix, "pool_scale": pool_scale, "w_pool_proj": w_pool_proj,
            "a_log": a_log, "dt_bias": dt_bias, "dn_head_norm": dn_head_norm,
            "w_dn_proj": w_dn_proj, "w_out": w_out, "final_norm_gain": final_norm_gain}


def reference(x, norm_gain, w_in, w_qkv_conv, pool_mix, pool_scale, w_pool_proj,
              a_log, dt_bias, dn_head_norm, w_dn_proj, w_out, final_norm_gain):
    b, s, _ = x.shape
    split_idx = [int(i) for i in np.cumsum(IN_SPLITS)[:-1]]
    h = x
    for layer in range(DEPTH):
        xn = rmsnorm(h, norm_gain[layer])
        proj = jnp.einsum("bsd,dc->bsc", xn, w_in[layer])
        (pool_u, pool_z, q, k, v, dn_z, dec_a, beta_b, gate_pool, gate_dn) = jnp.split(proj, split_idx, axis=-1)

        groups = jnp.split(pool_u, len(POOL_WINDOWS), axis=-1)
        pooled = jnp.stack([causal_mean_minus_self(u_g, w) for u_g, w in zip(groups, POOL_WINDOWS)], axis=2)
        mixed = jnp.einsum("bsgc,gcd->bsgd", pooled, pool_mix[layer]).reshape(b, s, POOL_WIDTH)
        y_pool = mixed * pool_scale[layer] * jax.nn.silu(pool_z)
        y_pool = jnp.einsum("bsc,cd->bsd", y_pool, w_pool_proj[layer])

        qkv = jax.nn.silu(causal_depthwise_conv(jnp.concatenate([q, k, v], axis=-1), w_qkv_conv[layer]))
        qkv = qkv.astype(jnp.float32)
        qh = l2norm(qkv[..., :DN_KEY].reshape(b, s, DN_HEADS, DN_HEAD_DIM)) * (DN_HEAD_DIM ** -0.5)
        kh = l2norm(qkv[..., DN_KEY:2 * DN_KEY].reshape(b, s, DN_HEADS, DN_HEAD_DIM))
        vh = qkv[..., 2 * DN_KEY:].reshape(b, s, DN_HEADS, DN_HEAD_DIM)
        g = -jnp.exp(a_log[layer].astype(jnp.float32)) * jax.nn.softplus(dec_a.astype(jnp.float32) + dt_bias[layer].astype(jnp.float32))
        beta = jax.nn.sigmoid(beta_b.astype(jnp.float32))
        o = gated_delta_rule_chunked(qh, kh, vh, g, beta)
        o = rmsnorm(o, dn_head_norm[layer]).reshape(b, s, DN_VAL).astype(x.dtype)
        y_dn = o * jax.nn.silu(dn_z)
        y_dn = jnp.einsum("bsc,cd->bsd", y_dn, w_dn_proj[layer])

        merged = jax.nn.sigmoid(gate_pool) * y_pool + jax.nn.sigmoid(gate_dn) * y_dn
        h = h + jnp.einsum("bsd,de->bse", merged, w_out[layer])
    return rmsnorm(h, final_norm_gain)
```

```python
from contextlib import ExitStack
import numpy as np
import ml_dtypes
import concourse.bass as bass
import concourse.mybir as mybir
from concourse.bass_utils import run_bass_kernel_spmd

F32 = mybir.dt.float32
BF16 = mybir.dt.bfloat16
AF = mybir.ActivationFunctionType
ALU = mybir.AluOpType
AX = mybir.AxisListType

D = 4096
S = 8192
KC = 32
NCORE = 8
TOK = 1024
TP = 512
HALO = 16
BIG = 30000.0
WDN_COLS = 1058
CV_GAIN = 0
CV_PSC = 32
CV_CONV = 64
CV_GN = 112
CV_ALOG = 113
CV_DTB = 115
CV_N = 120
CF_ONES = 0
CF_I2 = 128
CF_MASK = 256
CF_SEL = 640
CF_ID34 = 1152
CF_N = 1192


class Buf:
    def __init__(self, name, excl=False):
        self.name = name
        self.excl = excl
        self.w = {}
        self.r = {}


class Prog:
    ENG = ("pe", "act", "dve", "pool", "sp")

    def __init__(self, nc):
        self.nc = nc
        self.streams = {k: [] for k in self.ENG}
        self.sem = {}
        self.cnt = {}
        for k in ("pe", "act", "dve", "pool"):
            self.sem[k] = nc.alloc_semaphore("c_" + k)
            self.cnt[k] = 0
        self.seen = {k: {} for k in self.ENG}
        self.meta = {k: [] for k in self.ENG}

    def simulate(self):
        val = {k: 0 for k in self.sem}
        ptr = {k: 0 for k in self.ENG}
        while True:
            prog = False
            for e in self.ENG:
                m = self.meta[e]
                while ptr[e] < len(m):
                    t, k, n = m[ptr[e]]
                    if t == "w":
                        if val[k] >= n:
                            ptr[e] += 1
                            prog = True
                        else:
                            break
                    else:
                        val[k] += n
                        ptr[e] += 1
                        prog = True
            if all(ptr[e] == len(self.meta[e]) for e in self.ENG):
                return True
            if not prog:
                for e in self.ENG:
                    if ptr[e] < len(self.meta[e]):
                        print("STUCK", e, ptr[e], len(self.meta[e]), self.meta[e][ptr[e]], "val", val[self.meta[e][ptr[e]][1]])
                return False

    def dsem(self, key):
        if key not in self.sem:
            self.sem[key] = self.nc.alloc_semaphore("d_" + key)
            self.cnt[key] = 0
        return key

    def _waits(self, eng, reads, writes):
        need = {}
        for b in reads:
            for k, n in b.w.items():
                if k == eng and eng == "pe":
                    continue
                if n > need.get(k, 0):
                    need[k] = n
        for b in writes:
            for src in (b.w, b.r):
                for k, n in src.items():
                    if k == eng:
                        continue
                    if n > need.get(k, 0):
                        need[k] = n
        out = []
        for k, n in need.items():
            if self.seen[eng].get(k, 0) < n:
                self.seen[eng][k] = n
                out.append((k, n))
        return out

    def _emit_waits(self, eng, reads, writes):
        for k, n in self._waits(eng, reads, writes):
            s = self.sem[k]
            self.streams[eng].append(lambda e, s=s, n=n: e.wait_ge(s, n))
            self.meta[eng].append(("w", k, n))

    def op(self, eng, fn, reads=(), writes=()):
        ex = [b for b in reads if b.excl]
        if ex:
            reads = [b for b in reads if not b.excl]
            writes = list(writes) + ex
        self._emit_waits(eng, reads, writes)
        self.cnt[eng] += 1
        n = self.cnt[eng]
        s = self.sem[eng]
        self.streams[eng].append(lambda e, fn=fn, s=s: fn(e).then_inc(s, 1))
        self.meta[eng].append(("i", eng, 1))
        for b in reads:
            b.r[eng] = n
        for b in writes:
            b.w = {eng: n}
            b.r = {}

    def dma(self, q, key, out, in_, reads=(), writes=()):
        self.dsem(key)
        self._emit_waits(q, reads, writes)
        self.cnt[key] += 16
        n = self.cnt[key]
        s = self.sem[key]

        def f(e, out=out, in_=in_, s=s):
            o = out(e) if callable(out) else out
            i = in_(e) if callable(in_) else in_
            try:
                return e.dma_start(out=o, in_=i).then_inc(s, 16)
            except Exception:
                print("DMA FAIL", o, i)
                raise

        self.streams[q].append(f)
        self.meta[q].append(("i", key, 16))
        for b in reads:
            b.r[key] = n
        for b in writes:
            b.w = {key: n}
            b.r = {}

    def raw(self, eng, fn, meta=None):
        self.streams[eng].append(fn)
        if meta is not None:
            self.meta[eng].append(meta)

    def wait_all(self, eng):
        for k in list(self.sem.keys()):
            n = self.cnt[k]
            if n > 0 and self.seen[eng].get(k, 0) < n:
                self.seen[eng][k] = n
                s = self.sem[k]
                self.streams[eng].append(lambda e, s=s, n=n: e.wait_ge(s, n))
                self.meta[eng].append(("w", k, n))

    def barrier(self):
        for e in self.ENG:
            self.wait_all(e)

    def emit(self):
        nc = self.nc
        with nc.Block() as block:
            @block.tensor
            def _(e):
                for f in self.streams["pe"]:
                    f(e)

            @block.scalar
            def _(e):
                for f in self.streams["act"]:
                    f(e)

            @block.vector
            def _(e):
                for f in self.streams["dve"]:
                    f(e)

            @block.gpsimd
            def _(e):
                for f in self.streams["pool"]:
                    f(e)

            @block.sync
            def _(e):
                for f in self.streams["sp"]:
                    f(e)


_pid_cache = {}


def pid(e):
    k = id(e)
    if k not in _pid_cache:
        _pid_cache[k] = e.snap(e.partition_id(), min_val=0, max_val=NCORE - 1)
    return _pid_cache[k]


class RR:
    def __init__(self, engs):
        self.engs = engs
        self.i = 0

    def __call__(self):
        e = self.engs[self.i % len(self.engs)]
        self.i += 1
        return e


def copy_op(P, eng, out, in_, reads, writes):
    if eng == "act":
        P.op("act", lambda e: e.activation(out=out, in_=in_, func=AF.Copy), reads, writes)
    else:
        P.op(eng, lambda e: e.tensor_copy(out=out, in_=in_), reads, writes)


def xprep(P, T, nrows, src, xs, bxs, ssq, bssq, junk, bjunk, psums, dst_fn, bdst, evrr, key, src_bufs=()):
    P.dma("sp", key, xs[0:nrows, :], src, reads=list(src_bufs), writes=[bxs])
    if T.get("dbgx") is not None:
        P.dma("sp", "dbgx", T["dbgx"][0:nrows, :], xs[0:nrows, :], reads=[bxs])
        T["dbgx"] = None
    P.op("act", lambda e: e.activation(out=junk[0:nrows, :], in_=xs[0:nrows, :], func=AF.Square,
                                        accum_out=ssq[0:nrows, 0:1]), [bxs], [bjunk, bssq])
    P.op("act", lambda e: e.activation(out=ssq[0:nrows, 1:2], in_=ssq[0:nrows, 0:1], func=AF.Sqrt,
                                        scale=1.0 / D, bias=T["eps6"][0:nrows, 0:1]), [bssq], [bssq])
    P.op("dve", lambda e: e.reciprocal(out=ssq[0:nrows, 2:3], in_=ssq[0:nrows, 1:2]), [bssq], [bssq])
    P.op("dve", lambda e: e.tensor_scalar(out=xs[0:nrows, :], in0=xs[0:nrows, :], scalar1=ssq[0:nrows, 2:3],
                                          scalar2=None, op0=ALU.mult), [bxs, bssq], [bxs])
    identf = T["identf"]
    for g in range(KC // 4):
        ps, bps = psums[g % len(psums)]
        for j in range(4):
            k = g * 4 + j
            P.op("pe", lambda e, k=k, j=j, ps=ps: e.transpose(
                out=ps[:, j * 128:j * 128 + nrows], in_=xs[0:nrows, k * 128:(k + 1) * 128],
                identity=identf[0:nrows, 0:nrows]), [bxs], [bps])
        eng = evrr()
        src_ap = ps[:, :].rearrange("p (j t) -> p j t", j=4)[:, :, 0:nrows]
        copy_op(P, eng, dst_fn(g * 4, 4), src_ap, [bps], [bdst])


class _Cut(Exception):
    pass


def dn_phase(nc, P, A, n_hg=2, n_tt=S // 512):
    es = ExitStack()
    try:
        _dn_phase(nc, P, A, n_hg, n_tt, es)
    except _Cut:
        pass
    P.barrier()
    es.close()


def _dn_phase(nc, P, A, n_hg, n_tt, es):
    import os
    CUT = os.environ.get("KCUT", "")

    def cut(name, ap=None, bufs=()):
        if CUT == name:
            if ap is not None:
                npart, nfree = ap.shape[0], int(np.prod(ap.shape[1:]))
                dst = A["dbgb"] if ap.dtype == BF16 else A["dbg"]
                P.dma("sp", "dbg", dst[0:npart, 0:nfree], ap, reads=list(bufs))
            raise _Cut()

    def sb(name, shape, dt=F32):
        return es.enter_context(nc.sbuf_tensor("d_" + name, shape, dt)), Buf(name)

    def pp(name, shape, dt=F32):
        return es.enter_context(nc.psum_tensor("dp_" + name, shape, dt)), Buf(name, True)

    cf, bcf = sb("cf", [128, CF_N])
    identf_t, bidf = sb("identf", [128, 128])
    identb, bidb = sb("identb", [128, 128], BF16)
    cv, bcv = sb("cv", [128, CV_N])
    eps6, beps = sb("eps6", [128, 2])
    wbf, bwbf = sb("wbf", [128, KC, WDN_COLS], BF16)
    wst = [sb(f"wst{i}", [128, WDN_COLS]) for i in range(1)]
    xs, bxs = sb("xs", [128, D])
    junk, bjunk = sb("junk", [128, D], BF16)
    ssq, bssq = sb("ssq", [128, 4])
    xnT, bxnT = sb("xnT", [128, KC, 512], BF16)
    rawc = [sb(f"raw{i}", [128, 516]) for i in range(2)]
    hist, bhist = sb("hist", [128, 6, 4])
    cvt = [sb(f"cvt{i}", [128, 512]) for i in range(1)]
    qs = [sb(f"qs{i}", [128, 512]) for i in range(1)]
    sqt, bsq = sb("sq", [128, 512])
    kq, _ = sb("kq", [128, 2, 2, 512], BF16)
    bkq = [[Buf(f"kq{h}{t}") for t in range(2)] for h in range(2)]
    qg, bqg = sb("qg", [128, 2, 512], BF16)
    vT, _ = sb("vT", [128, 2, 512], BF16)
    bvT = [Buf("vT0"), Buf("vT1")]
    zs, _ = sb("zs", [128, 2, 512])
    bzs = [Buf("zs0"), Buf("zs1")]
    Rr, bRr = sb("Rr", [34, 512])
    tmpr, btmpr = sb("tmpr", [34, 512])
    rmask, brmask = sb("rmask", [2, 512])
    nA, bnA = sb("nA", [2, 4])
    bc, _ = sb("bc", [128, 3, 2, 512])
    bbc = [[Buf(f"bc{j}{h}") for h in range(2)] for j in range(3)]
    egc, begc = sb("egc", [128, 2, 512])
    osb, bosb = sb("osb", [128, 2, 512])
    og = [sb(f"og{i}", [128, 2, 512], BF16) for i in range(2)]
    ccol = [sb(f"ccol{i}", [64, 34]) for i in range(2)]
    cols = [sb(f"cols{i}", [64, 12]) for i in range(2)]
    kbg = [sb(f"kbg{i}", [64, 2, 128], BF16) for i in range(2)]
    kdec = [sb(f"kdec{i}", [64, 2, 128], BF16) for i in range(2)]
    vb = [sb(f"vb{i}", [64, 2, 128], BF16) for i in range(2)]
    Xt = [sb(f"X{i}", [64, 2, 3, 64]) for i in range(2)]
    Et = [sb(f"E{i}", [64, 2, 3, 64]) for i in range(2)]
    PR = [sb(f"PR{i}", [64, 2, 2, 64]) for i in range(2)]
    PT = [sb(f"PT{i}", [64, 2, 64]) for i in range(2)]
    Tt, bTt = sb("Tt", [64, 2, 64], BF16)
    qkT = [sb(f"qkT{i}", [64, 2, 64], BF16) for i in range(2)]
    nwT, bnwT = sb("nwT", [128, 2, 64], BF16)
    vn, bvn = sb("vn", [64, 2, 128], BF16)
    Sf, bSf = sb("Sf", [128, 2, 128])
    Sb = [sb(f"Sb{i}", [128, 2, 128], BF16) for i in range(2)]
    pA = [pp("pA0", [128, 512]), pp("pA1", [128, 512])]
    pT = [pp("pT0", [128, 512])]
    pBC, bpBC = pp("pBC", [128, 512])
    pK, bpK = pp("pK", [128, 1024], BF16)
    pC, bpC_ = pp("pC", [128, 512])
    pI, bpI_ = pp("pI", [128, 512])
    pV, bpV_ = pp("pV", [128, 512])
    bpCc = bpGQ = bpW = bpC_
    bpPR = bpPT = bpO = bpI_
    bpVv = bpdS = bpV_

    T = {"identf": identf_t, "eps6": eps6, "dbgx": A.get("dbgx")}
    evrr = RR(["act", "dve"])

    P.dma("sp", "c0", cf[:, :], A["cf32"][:, :], writes=[bcf])
    P.dma("sp", "c1", identf_t[:, :], A["identf"][:, :], writes=[bidf])
    P.dma("sp", "c2", identb[:, :], A["identb"][:, :], writes=[bidb])
    P.dma("sp", "c3", cv[:, :], A["cvec"][:, :], writes=[bcv])
    P.op("dve", lambda e: e.memset(eps6[:, 0:1], 1e-6), [], [beps])
    P.op("dve", lambda e: e.memset(rmask[:, :], 1.0), [], [brmask])
    P.op("dve", lambda e: e.memset(rmask[:, :].rearrange("p (c t) -> p c t", t=64)[:, :, 0:1], 0.0), [], [brmask])
    P.op("dve", lambda e: e.memset(Rr[:, :], 0.0), [], [bRr])
    P.op("dve", lambda e: e.memset(hist[:, :, :], 0.0), [], [bhist])
    P.op("act", lambda e: e.activation(out=nA[:, 0:2], in_=cv[0:2, CV_ALOG:CV_ALOG + 2], func=AF.Exp), [bcv], [bnA])
    P.op("dve", lambda e: e.tensor_scalar(out=nA[:, 2:4], in0=nA[:, 0:2], scalar1=-1.0, scalar2=None, op0=ALU.mult),
         [bnA], [bnA])

    ones = cf[:, CF_ONES:CF_ONES + 128]
    I2 = cf[0:64, CF_I2:CF_I2 + 128].rearrange("p (h f) -> p h f", h=2)
    MSK = cf[0:64, CF_MASK:CF_MASK + 384].rearrange("p (h j f) -> p h j f", h=2, j=3)
    id34 = cf[0:34, CF_ID34:CF_ID34 + 34]

    def sel(j, h):
        if j < 2:
            o = CF_SEL + j * 128 + h * 64
            return cf[0:34, o:o + 64], 64
        o = CF_SEL + 256 + h * 128
        return cf[0:34, o:o + 128], 128

    ogst_i = [0]
    bsend = [[Buf(f"ogs{h}_{t}") for t in range(16)] for h in range(2)]

    for hg in range(n_hg):
        for k in range(KC):
            w_t, bw = wst[0]
            P.dma("sp", "wst0", w_t[:, :], A["wdn"][hg, k * 128:(k + 1) * 128, :], writes=[bw])
            eng = evrr()
            if eng == "act":
                P.op("act", lambda e, k=k, w_t=w_t: e.activation(out=wbf[:, k, :], in_=w_t[:, :], func=AF.Identity,
                                                              scale=cv[:, CV_GAIN + k:CV_GAIN + k + 1]),
                     [bw, bcv], [bwbf])
            else:
                P.op("dve", lambda e, k=k, w_t=w_t: e.tensor_scalar(out=wbf[:, k, :], in0=w_t[:, :],
                                                                 scalar1=cv[:, CV_GAIN + k:CV_GAIN + k + 1],
                                                                 scalar2=None, op0=ALU.mult),
                     [bw, bcv], [bwbf])
        P.op("dve", lambda e: e.memset(Sf[:, :, :], 0.0), [], [bSf])
        P.op("dve", lambda e: e.memset(Sb[0][0][:, :, :], 0.0), [], [Sb[0][1]])
        P.op("dve", lambda e: e.memset(hist[:, :, :], 0.0), [], [bhist])
        sb_i = 0

        for tt in range(n_tt):
            for s4 in range(4):
                r0 = tt * 512 + s4 * 128
                xprep(P, T, 128, A["xg"][r0:r0 + 128, :], xs, bxs, ssq, bssq, junk, bjunk, pT + pA,
                      lambda k0, nk, s4=s4: xnT[:, k0:k0 + nk, s4 * 128:(s4 + 1) * 128], bxnT, evrr, "xld",
                      src_bufs=[A["b_xg"][r0 // 256]])
            cut("A", xnT[:, 0:4, :], [bxnT])
            for b in range(9):
                ps, bps = pA[b % 2]
                if b < 8:
                    M, c0 = 128, b * 128
                else:
                    M, c0 = 34, 1024
                for k in range(KC):
                    P.op("pe", lambda e, k=k, ps=ps, M=M, c0=c0: e.matmul(
                        ps[0:M, :], lhsT=wbf[:, k, c0:c0 + M], rhs=xnT[:, k, :], start=(k == 0), stop=(k == KC - 1)),
                        [bwbf, bxnT], [bps])
                if b < 6:
                    typ, h = b // 2, b % 2
                    rw, brw = rawc[b % 2]
                    cvb, bcvb = cvt[0]
                    P.op("act", lambda e, rw=rw, ps=ps: e.activation(out=rw[:, 3:515], in_=ps[:, :], func=AF.Copy),
                         [bps], [brw])
                    P.op("dve", lambda e, rw=rw, b=b: e.tensor_copy(out=rw[:, 0:3], in_=hist[:, b, 0:3]), [bhist], [brw])
                    cw = CV_CONV + (hg * 6 + b) * 4
                    P.op("dve", lambda e, rw=rw, cvb=cvb, cw=cw: e.tensor_scalar(
                        out=cvb[:, :], in0=rw[:, 0:512], scalar1=cv[:, cw:cw + 1], scalar2=None, op0=ALU.mult),
                        [brw, bcv], [bcvb])
                    for j in range(1, 4):
                        P.op("dve", lambda e, rw=rw, cvb=cvb, cw=cw, j=j: e.scalar_tensor_tensor(
                            out=cvb[:, :], in0=rw[:, j:j + 512], scalar=cv[:, cw + j:cw + j + 1], in1=cvb[:, :],
                            op0=ALU.mult, op1=ALU.add), [brw, bcv, bcvb], [bcvb])
                    P.op("dve", lambda e, rw=rw, b=b: e.tensor_copy(out=hist[:, b, 0:3], in_=rw[:, 512:515]), [brw], [bhist])
                    if typ == 2:
                        P.op("act", lambda e, cvb=cvb, h=h: e.activation(out=vT[:, h, :], in_=cvb[:, :], func=AF.Silu),
                             [bcvb], [bvT[h]])
                    else:
                        q_t, bq = qs[0]
                        P.op("act", lambda e, cvb=cvb, q_t=q_t: e.activation(out=q_t[:, :], in_=cvb[:, :], func=AF.Silu),
                             [bcvb], [bq])
                        P.op("dve", lambda e, q_t=q_t: e.tensor_tensor(out=sqt[:, :], in0=q_t[:, :], in1=q_t[:, :], op=ALU.mult),
                             [bq], [bsq])
                        P.op("pe", lambda e: e.matmul(pBC[:, :], lhsT=ones, rhs=sqt[:, :], start=True, stop=True),
                             [bcf, bsq], [bpBC])
                        P.op("act", lambda e: e.activation(out=sqt[:, :], in_=pBC[:, :], func=AF.Sqrt, bias=eps6[:, 0:1]),
                             [bpBC, beps], [bsq])
                        P.op("dve", lambda e: e.reciprocal(out=sqt[:, :], in_=sqt[:, :]), [bsq], [bsq])
                        sc = (128.0 ** -0.5) if typ == 0 else 1.0
                        slot = 1 if typ == 0 else 0
                        P.op("dve", lambda e, q_t=q_t, h=h, slot=slot, sc=sc: e.scalar_tensor_tensor(
                            out=kq[:, h, slot, :], in0=q_t[:, :], scalar=sc, in1=sqt[:, :], op0=ALU.mult, op1=ALU.mult),
                            [bq, bsq], [bkq[h][slot]])
                elif b < 8:
                    h = b - 6
                    P.op("act", lambda e, ps=ps, h=h: e.activation(out=zs[:, h, :], in_=ps[:, :], func=AF.Silu),
                         [bps], [bzs[h]])
                else:
                    P.op("act", lambda e, ps=ps: e.activation(out=tmpr[:, :], in_=ps[0:34, :], func=AF.Copy), [bps], [btmpr])
                    P.op("act", lambda e, hg=hg: e.activation(out=tmpr[0:2, :], in_=tmpr[0:2, :], func=AF.Exp,
                                                               bias=cv[0:2, CV_DTB + hg:CV_DTB + hg + 1]), [btmpr, bcv], [btmpr])
                    P.op("act", lambda e: e.activation(out=tmpr[32:34, :], in_=tmpr[32:34, :], func=AF.Exp, scale=-1.0),
                         [btmpr], [btmpr])
                    P.op("dve", lambda e: e.tensor_scalar(out=tmpr[0:2, :], in0=tmpr[0:2, :], scalar1=1.0, scalar2=None, op0=ALU.add),
                         [btmpr], [btmpr])
                    P.op("dve", lambda e: e.tensor_scalar(out=tmpr[32:34, :], in0=tmpr[32:34, :], scalar1=1.0, scalar2=None, op0=ALU.add),
                         [btmpr], [btmpr])
                    P.op("act", lambda e: e.activation(out=tmpr[0:2, :], in_=tmpr[0:2, :], func=AF.Ln), [btmpr], [btmpr])
                    P.op("act", lambda e: e.activation(out=tmpr[32:34, :], in_=tmpr[32:34, :], func=AF.Ln), [btmpr], [btmpr])
                    P.op("dve", lambda e, hg=hg: e.tensor_scalar(out=tmpr[0:2, :], in0=tmpr[0:2, :], scalar1=nA[:, 2 + hg:3 + hg],
                                                          scalar2=None, op0=ALU.mult), [btmpr, bnA], [btmpr])
                    P.op("dve", lambda e: e.tensor_scalar(out=Rr[32:34, :], in0=tmpr[32:34, :], scalar1=-1.0, scalar2=None,
                                                          op0=ALU.mult), [btmpr], [bRr])
                    P.op("dve", lambda e: e.tensor_tensor_scan(out=Rr[0:2, :], data0=rmask[:, :], data1=tmpr[0:2, :],
                                                               initial=0.0, op0=ALU.mult, op1=ALU.add),
                         [btmpr, brmask], [bRr])
                    P.op("dve", lambda e: e.tensor_copy(out=tmpr[0:2, :], in_=Rr[0:2, :]), [bRr], [btmpr, bRr])
            cut("B", kq[:, :, :, :], [bkq[0][0], bkq[0][1], bkq[1][0], bkq[1][1]])
            cut("B2", Rr[:, :], [bRr])
            cut("B3", vT[:, :, :], bvT)
            for j in range(3):
                for h in range(2):
                    lh, M = sel(j, h)
                    P.op("pe", lambda e, lh=lh, M=M: e.matmul(pBC[0:M, :], lhsT=lh, rhs=Rr[:, :], start=True, stop=True),
                         [bcf, bRr], [bpBC])
                    copy_op(P, evrr(), bc[0:M, j, h, :], pBC[0:M, :], [bpBC], [bbc[j][h]])
            P.op("act", lambda e: e.activation(out=egc[:, :, :], in_=bc[:, 2, :, :], func=AF.Exp), bbc[2], [begc])
            P.op("dve", lambda e: e.tensor_tensor(out=qg[:, :, :], in0=kq[:, :, 1, :], in1=egc[:, :, :], op=ALU.mult),
                 [bkq[0][1], bkq[1][1], begc], [bqg])

            cut("BC", bc[:, 2, :, :], bbc[2])
            for c in range(8):
                cs = slice(c * 64, (c + 1) * 64)
                i2 = c % 2
                cc_t, bcc = ccol[i2]
                co_t, bco = cols[i2]
                kbg_t, bkbg = kbg[i2]
                kd_t, bkd = kdec[i2]
                vb_t, bvb = vb[i2]
                X_t, bX = Xt[i2]
                E_t, bE = Et[i2]
                qk_t, bqk = qkT[i2]
                P.op("pe", lambda e, cs=cs: e.transpose(out=pC[0:64, 0:34], in_=Rr[:, cs], identity=id34), [bRr, bcf], [bpCc])
                for h in range(2):
                    P.op("pe", lambda e, cs=cs, h=h: e.transpose(out=pK[0:64, h * 128:(h + 1) * 128], in_=kq[:, h, 0, cs],
                                                             identity=identb[:, :]), [bkq[h][0], bidb], [bpK])
                    P.op("pe", lambda e, cs=cs, h=h: e.transpose(out=pK[0:64, (2 + h) * 128:(3 + h) * 128], in_=vT[:, h, cs],
                                                             identity=identb[:, :]), [bvT[h], bidb], [bpK])
                P.op("act", lambda e, cc_t=cc_t: e.activation(out=cc_t[:, :], in_=pC[0:64, 0:34], func=AF.Copy), [bpCc], [bcc])
                P.op("dve", lambda e, cc_t=cc_t, co_t=co_t: e.tensor_tensor(out=co_t[:, 0:2], in0=cc_t[:, 0:2], in1=cc_t[:, 32:34], op=ALU.add),
                     [bcc], [bco])
                P.op("dve", lambda e, co_t=co_t: e.tensor_scalar(out=co_t[:, 2:4], in0=co_t[:, 0:2], scalar1=-1.0, scalar2=None, op0=ALU.mult),
                     [bco], [bco])
                P.op("dve", lambda e, cc_t=cc_t, co_t=co_t: e.tensor_scalar(out=co_t[:, 10:12], in0=cc_t[:, 0:2], scalar1=-1.0, scalar2=None, op0=ALU.mult),
                     [bcc], [bco])
                P.op("dve", lambda e, cc_t=cc_t, co_t=co_t, c=c: e.tensor_tensor(
                    out=co_t[:, 8:10], in0=bc[0:64, 2, :, c * 64 + 63], in1=cc_t[:, 0:2], op=ALU.subtract),
                    [bcc, bbc[2][0], bbc[2][1]], [bco])
                P.op("act", lambda e, co_t=co_t: e.activation(out=co_t[:, 4:6], in_=co_t[:, 0:2], func=AF.Exp), [bco], [bco])
                P.op("act", lambda e, co_t=co_t, cc_t=cc_t: e.activation(out=co_t[:, 6:8], in_=cc_t[:, 32:34], func=AF.Exp), [bcc, bco], [bco])
                P.op("act", lambda e, co_t=co_t: e.activation(out=co_t[:, 8:10], in_=co_t[:, 8:10], func=AF.Exp), [bco], [bco])
                for h in range(2):
                    P.op("dve", lambda e, h=h, kbg_t=kbg_t, co_t=co_t: e.tensor_scalar(
                        out=kbg_t[:, h, :], in0=pK[0:64, h * 128:(h + 1) * 128], scalar1=co_t[:, 4 + h:5 + h], scalar2=None, op0=ALU.mult),
                        [bpK, bco], [bkbg])
                    P.op("dve", lambda e, h=h, kd_t=kd_t, co_t=co_t: e.tensor_scalar(
                        out=kd_t[:, h, :], in0=pK[0:64, h * 128:(h + 1) * 128], scalar1=co_t[:, 8 + h:9 + h], scalar2=None, op0=ALU.mult),
                        [bpK, bco], [bkd])
                    P.op("act", lambda e, h=h, vb_t=vb_t, co_t=co_t: e.activation(
                        out=vb_t[:, h, :], in_=pK[0:64, (2 + h) * 128:(3 + h) * 128], func=AF.Identity, scale=co_t[:, 6 + h:7 + h]),
                        [bpK, bco], [bvb])
                cut("C1", kbg_t[:, :, :], [bkbg, bkd, bvb])
                for h in range(2):
                    P.op("pe", lambda e, h=h, cs=cs: e.matmul(pC[0:64, 64 + h * 128:64 + (h + 1) * 128], lhsT=kq[:, h, 0, cs],
                                                              rhs=kq[:, h, :, cs], start=True, stop=True),
                         [bkq[h][0], bkq[h][1]], [bpGQ])
                for h in range(2):
                    P.op("dve", lambda e, h=h, X_t=X_t, cc_t=cc_t, cs=cs: e.scalar_tensor_tensor(
                        out=X_t[:, h, 0, :], in0=bc[0:64, 0, h, cs], scalar=cc_t[:, h:h + 1], in1=MSK[:, h, 0, :],
                        op0=ALU.subtract, op1=ALU.add), [bbc[0][h], bcc, bcf], [bX])
                    P.op("dve", lambda e, h=h, X_t=X_t, co_t=co_t, cs=cs: e.scalar_tensor_tensor(
                        out=X_t[:, h, 1, :], in0=bc[0:64, 1, h, cs], scalar=co_t[:, 2 + h:3 + h], in1=MSK[:, h, 1, :],
                        op0=ALU.subtract, op1=ALU.add), [bbc[1][h], bco, bcf], [bX])
                    P.op("dve", lambda e, h=h, X_t=X_t, cc_t=cc_t, cs=cs: e.scalar_tensor_tensor(
                        out=X_t[:, h, 2, :], in0=bc[0:64, 2, h, cs], scalar=cc_t[:, h:h + 1], in1=MSK[:, h, 2, :],
                        op0=ALU.subtract, op1=ALU.add), [bbc[2][h], bcc, bcf], [bX])
                P.op("act", lambda e, X_t=X_t, E_t=E_t: e.activation(out=E_t[:, :, :, :], in_=X_t[:, :, :, :], func=AF.Exp), [bX], [bE])
                cut("C2", E_t[:, :, :, :], [bE])
                GQ = pC[0:64, 64:320].rearrange("p (h t f) -> p h t f", h=2, t=2)
                pa, bpa = PR[0]
                pb, bpb = PR[1]
                ta, bta = PT[0]
                tb, btb = PT[1]
                P.op("dve", lambda e, E_t=E_t: e.tensor_tensor(out=pa[:, :, 0, :], in0=GQ[:, :, 0, :], in1=E_t[:, :, 0, :], op=ALU.mult),
                     [bpGQ, bE], [bpa])
                P.op("dve", lambda e, E_t=E_t: e.tensor_tensor(out=ta[:, :, :], in0=GQ[:, :, 0, :], in1=E_t[:, :, 1, :], op=ALU.mult),
                     [bpGQ, bE], [bta])
                P.op("dve", lambda e, E_t=E_t, qk_t=qk_t: e.tensor_tensor(out=qk_t[:, :, :], in0=GQ[:, :, 1, :], in1=E_t[:, :, 2, :], op=ALU.mult),
                     [bpGQ, bE], [bqk])
                P.op("dve", lambda e: e.tensor_tensor(out=pa[:, :, 1, :], in0=I2, in1=pa[:, :, 0, :], op=ALU.subtract),
                     [bcf, bpa], [bpa])
                cut("C3", pa[:, :, :, :], [bpa, bta, bqk])
                cur, bcur, nxt, bnxt = pa, bpa, pb, bpb
                tcur, btcur, tnxt, btnxt = ta, bta, tb, btb
                pPR = pI[0:64, 0:256].rearrange("p (h t f) -> p h t f", h=2, t=2)
                pPT = pI[0:64, 256:384].rearrange("p (h f) -> p h f", h=2)
                for lvl in range(6):
                    for h in range(2):
                        if lvl == 0:
                            P.op("pe", lambda e, h=h, cur=cur, tcur=tcur: e.matmul(
                                pI[0:64, h * 128:h * 128 + 64], lhsT=tcur[:, h, :], rhs=cur[:, h, 0, :], start=True, stop=True),
                                [bcur, btcur], [bpPR])
                        elif lvl < 5:
                            P.op("pe", lambda e, h=h, cur=cur, tcur=tcur: e.matmul(
                                pI[0:64, h * 128:(h + 1) * 128], lhsT=tcur[:, h, :], rhs=cur[:, h, :, :], start=True, stop=True),
                                [bcur, btcur], [bpPR])
                        else:
                            P.op("pe", lambda e, h=h, cur=cur, tcur=tcur: e.matmul(
                                pI[0:64, h * 128 + 64:(h + 1) * 128], lhsT=tcur[:, h, :], rhs=cur[:, h, 1, :], start=True, stop=True),
                                [bcur, btcur], [bpPR])
                        if lvl < 5:
                            P.op("pe", lambda e, h=h, cur=cur, tcur=tcur: e.matmul(
                                pI[0:64, 256 + h * 64:256 + (h + 1) * 64], lhsT=cur[:, h, 0, :], rhs=tcur[:, h, :], start=True, stop=True),
                                [bcur, btcur], [bpPT])
                    if lvl < 5:
                        P.op("act", lambda e, nxt=nxt: e.activation(out=nxt[:, :, 0, :], in_=pPR[:, :, 0, :], func=AF.Copy), [bpPR], [bnxt])
                        P.op("act", lambda e, tnxt=tnxt: e.activation(out=tnxt[:, :, :], in_=pPT, func=AF.Copy), [bpPT], [btnxt])
                    if lvl == 0:
                        P.op("dve", lambda e, nxt=nxt, cur=cur: e.tensor_copy(out=nxt[:, :, 1, :], in_=cur[:, :, 1, :]), [bcur], [bnxt])
                    elif lvl < 5:
                        P.op("dve", lambda e, nxt=nxt, cur=cur: e.tensor_tensor(out=nxt[:, :, 1, :], in0=pPR[:, :, 1, :], in1=cur[:, :, 1, :], op=ALU.add),
                             [bpPR, bcur], [bnxt])
                    else:
                        P.op("dve", lambda e, cur=cur: e.tensor_tensor(out=Tt[:, :, :], in0=pPR[:, :, 1, :], in1=cur[:, :, 1, :], op=ALU.add),
                             [bpPR, bcur], [bTt])
                    cur, bcur, nxt, bnxt = nxt, bnxt, cur, bcur
                    tcur, btcur, tnxt, btnxt = tnxt, btnxt, tcur, btcur
                cut("INV", Tt[:, :, :], [bTt])
                S_t, bS = Sb[sb_i]
                S_n, bSn = Sb[1 - sb_i]
                for h in range(2):
                    P.op("pe", lambda e, h=h, kbg_t=kbg_t: e.matmul(pC[:, 320 + h * 64:320 + (h + 1) * 64], lhsT=kbg_t[:, h, :], rhs=Tt[:, h, :],
                                                                    start=True, stop=True), [bkbg, bTt], [bpW])
                P.op("act", lambda e: e.activation(out=nwT[:, :, :], in_=pC[:, 320:448].rearrange("p (h f) -> p h f", h=2), func=AF.Copy, scale=-1.0),
                     [bpW], [bnwT])
                for h in range(2):
                    P.op("pe", lambda e, h=h, vb_t=vb_t: e.matmul(pV[0:64, h * 128:(h + 1) * 128], lhsT=Tt[:, h, :], rhs=vb_t[:, h, :],
                                                                  start=True, stop=False), [bTt, bvb], [bpVv])
                    P.op("pe", lambda e, h=h, S_t=S_t: e.matmul(pV[0:64, h * 128:(h + 1) * 128], lhsT=nwT[:, h, :], rhs=S_t[:, h, :],
                                                                start=False, stop=True), [bnwT, bS], [bpVv])
                P.op("act", lambda e: e.activation(out=vn[:, :, :], in_=pV[0:64, 0:256].rearrange("p (h f) -> p h f", h=2), func=AF.Copy),
                     [bpVv], [bvn])
                for h in range(2):
                    P.op("pe", lambda e, h=h, S_t=S_t, cs=cs: e.matmul(pI[:, 384 + h * 64:384 + (h + 1) * 64], lhsT=S_t[:, h, :], rhs=qg[:, h, cs],
                                                                      start=True, stop=False), [bS, bqg], [bpO])
                    P.op("pe", lambda e, h=h, qk_t=qk_t: e.matmul(pI[:, 384 + h * 64:384 + (h + 1) * 64], lhsT=vn[:, h, :], rhs=qk_t[:, h, :],
                                                                  start=False, stop=True), [bvn, bqk], [bpO])
                P.op("act", lambda e, cs=cs: e.activation(out=osb[:, :, cs], in_=pI[:, 384:512].rearrange("p (h f) -> p h f", h=2), func=AF.Copy),
                     [bpO], [bosb])
                for h in range(2):
                    P.op("pe", lambda e, h=h, kd_t=kd_t: e.matmul(pV[:, 256 + h * 128:256 + (h + 1) * 128], lhsT=kd_t[:, h, :], rhs=vn[:, h, :],
                                                                  start=True, stop=True), [bkd, bvn], [bpdS])
                for h in range(2):
                    P.op("dve", lambda e, h=h, c=c: e.scalar_tensor_tensor(
                        out=Sf[:, h, :], in0=Sf[:, h, :], scalar=egc[:, h, c * 64 + 63:c * 64 + 64], in1=pV[:, 256 + h * 128:256 + (h + 1) * 128],
                        op0=ALU.mult, op1=ALU.add), [bSf, begc, bpdS], [bSf])
                P.op("act", lambda e, S_n=S_n: e.activation(out=S_n[:, :, :], in_=Sf[:, :, :], func=AF.Copy), [bSf], [bSn])
                sb_i = 1 - sb_i

            cut("C", osb[:, :, :], [bosb])
            og_t, bog = og[ogst_i[0] % 2]
            for h in range(2):
                P.op("dve", lambda e, h=h: e.tensor_tensor(out=sqt[:, :], in0=osb[:, h, :], in1=osb[:, h, :], op=ALU.mult), [bosb], [bsq])
                P.op("pe", lambda e: e.matmul(pBC[:, :], lhsT=ones, rhs=sqt[:, :], start=True, stop=True), [bcf, bsq], [bpBC])
                P.op("act", lambda e: e.activation(out=sqt[:, :], in_=pBC[:, :], func=AF.Sqrt, scale=1.0 / 128, bias=eps6[:, 0:1]),
                     [bpBC, beps], [bsq])
                P.op("dve", lambda e: e.reciprocal(out=sqt[:, :], in_=sqt[:, :]), [bsq], [bsq])
                P.op("dve", lambda e, h=h: e.tensor_tensor(out=sqt[:, :], in0=sqt[:, :], in1=osb[:, h, :], op=ALU.mult), [bsq, bosb], [bsq])
                P.op("dve", lambda e, h=h, og_t=og_t: e.scalar_tensor_tensor(
                    out=og_t[:, h, :], in0=sqt[:, :], scalar=cv[:, CV_GN:CV_GN + 1], in1=zs[:, h, :], op0=ALU.mult, op1=ALU.mult),
                    [bsq, bcv, bzs[h]], [bog])
            dst = A["og_send"][tt].rearrange("(h p) t -> p h t", p=128)[:, 2 * hg:2 * hg + 2, :]
            bst_og = bsend[hg][tt]
            P.dma("sp", f"ogst{ogst_i[0] % 2}", dst, og_t[:, :, :], reads=[bog], writes=[bst_og])
            ogst_i[0] += 1
            if hg == 1:
                P._emit_waits("pool", [bsend[0][tt], bsend[1][tt]], [])
                P.dsem("cc")
                P.cnt["cc"] += 1
                P.raw("pool", lambda e, tt=tt: e.collective_compute(
                    "AllGather", ALU.bypass, replica_groups=[list(range(NCORE))], ins=[A["og_send32"][tt]],
                    outs=[A["og_all32"][tt]]).then_inc(P.sem["cc"], 1), meta=("i", "cc", 1))
                if tt == n_tt - 1:
                    P.cnt["cc"] += 1
                    P.raw("pool", lambda e, tt=tt: e.collective_compute(
                        "AllGather", ALU.bypass, replica_groups=[list(range(NCORE))], ins=[A["og_send32"][tt]],
                        outs=[A["og_dummy"][:, :]]).then_inc(P.sem["cc"], 1), meta=("i", "cc", 1))


def token_phase(nc, P, A):
    es = ExitStack()

    def sb(name, shape, dt=F32):
        return es.enter_context(nc.sbuf_tensor("t_" + name, shape, dt)), Buf(name)

    identf_t, bidf = sb("identf2", [128, 128])
    cv, bcv = sb("cv2", [128, CV_N])
    eps6, beps = sb("eps62", [128, 2])
    xs, bxs = sb("xs2", [128, D])
    junk, bjunk = sb("junk2", [128, D], BF16)
    ssq, bssq = sb("ssq2", [128, 4])
    xnT, bxnT = sb("xnT2", [128, KC, HALO + TP], BF16)
    pooledT, bpool = sb("pooledT", [128, 8, TP], BF16)
    fwork, _ = sb("fwork", [128, 8, TP])
    bfw = [Buf(f"fw{i}") for i in range(8)]
    ypre, bypre = sb("ypre", [128, KC, TP], BF16)
    ogT, bogT = sb("ogT", [128, KC, TP], BF16)
    mrg, bmrg = sb("mrg", [128, KC, TP], BF16)
    sg = [sb(f"sg{i}", [128, TP]) for i in range(2)]
    NS = 3
    wsf = [sb(f"wsf{i}", [128, 512]) for i in range(NS)]
    wsb = [sb(f"wsb{i}", [128, 512], BF16) for i in range(NS)]
    rc16, brc = sb("rc16", [128, 4, 16])
    vmt, bvm = sb("vmt", [128, 2, 32])
    ssqp, bssqp = sb("ssqp", [128, 4, 8])
    rstd4, brstd4 = sb("rstd4", [128, 8])
    acc = [(es.enter_context(nc.psum_tensor(f"tp_acc{i}", [128, 512], F32)), Buf(f"acc{i}", True)) for i in range(8)]

    T = {"identf": identf_t, "eps6": eps6}
    evrr = RR(["act", "dve"])
    castrr = RR(["pool", "act", "pool", "dve"])

    P.dma("sp", "c1", identf_t[:, :], A["identf"][:, :], writes=[bidf])
    P.dma("sp", "c3", cv[:, :], A["cvec"][:, :], writes=[bcv])
    P.op("dve", lambda e: e.memset(eps6[:, 0:1], 1e-6), [], [beps])

    uext = [(xs[:, i * 528:(i + 1) * 528], Buf(f"uext{i}")) for i in range(2)]
    sa = (xs[:, 1056:1584], Buf("sa"))
    sbb = (xs[:, 1584:2112], Buf("sbb"))
    hsb = [(xs[:, 2112 + i * 512:2112 + (i + 1) * 512], Buf(f"hsb{i}")) for i in range(2)]
    fg32 = fwork[:, :, :].rearrange("p a b -> p (a b)")
    bfg = Buf("fg")
    xr = [sb(f"xr{i}", [128, 512]) for i in range(2)]

    slab_i = [0]

    def stream_group(wsrc, nk, gain, rhs_fn, banks, extra_fn=None, wbuf=None):
        def load(k):
            i = (slab_i[0] + k) % NS
            P.dma("sp", f"wsf{i}", wsf[i][0][:, :], wsrc(k), reads=wbuf(k), writes=[wsf[i][1]])

        def cast(k):
            i = (slab_i[0] + k) % NS
            src, bsrc = wsf[i]
            dst, bdst = wsb[i]
            eng = castrr()
            if gain:
                gcol = cv[:, CV_GAIN + k:CV_GAIN + k + 1]
                if eng == "act":
                    P.op("act", lambda e: e.activation(out=dst[:, :], in_=src[:, :], func=AF.Identity, scale=gcol), [bsrc, bcv], [bdst])
                else:
                    P.op(eng, lambda e: e.tensor_scalar(out=dst[:, :], in0=src[:, :], scalar1=gcol, scalar2=None, op0=ALU.mult),
                         [bsrc, bcv], [bdst])
            else:
                copy_op(P, eng, dst[:, :], src[:, :], [bsrc], [bdst])

        def mm(k):
            i = (slab_i[0] + k) % NS
            dst, bdst = wsb[i]
            rhs, brhs = rhs_fn(k)
            for j in range(4):
                a_t, ba = acc[banks[j]]
                P.op("pe", lambda e, j=j, a_t=a_t, rhs=rhs: e.matmul(a_t[:, :], lhsT=dst[:, j * 128:(j + 1) * 128], rhs=rhs,
                                                                   start=(k == 0), stop=(k == nk - 1)), [bdst] + brhs, [ba])
            if extra_fn is not None:
                extra_fn(k, dst, bdst)

        load(0)
        if nk > 1:
            load(1)
        cast(0)
        for k in range(nk):
            if k + 2 < nk:
                load(k + 2)
            if k + 1 < nk:
                cast(k + 1)
            mm(k)
        slab_i[0] += nk

    gflip = [0]

    def banks_next():
        b = [0, 1, 2, 3] if gflip[0] % 2 == 0 else [4, 5, 6, 7]
        gflip[0] += 1
        return b

    for p in range(2):
        base = p * TP
        def xrows(off, n):
            return A["xown"][off:off + n, :]

        xprep(P, T, HALO, xrows(base, HALO), xs, bxs, ssq, bssq, junk, bjunk, acc,
              lambda k0, nk: xnT[:, k0:k0 + nk, 0:HALO], bxnT, evrr, "xld")
        for s4 in range(4):
            xprep(P, T, 128, xrows(base + HALO + s4 * 128, 128), xs, bxs, ssq, bssq, junk, bjunk, acc,
                  lambda k0, nk, s4=s4: xnT[:, k0:k0 + nk, HALO + s4 * 128:HALO + (s4 + 1) * 128], bxnT, evrr, "xld")
        P.dma("sp", "vm", vmt[:, 0, :], A["vmask"][0:1, base:base + 32].partition_broadcast(128),
              writes=[bvm])
        P.barrier()
        sh = 1
        for g in range(4):
            a, b = g % 2, 1 - g % 2
            P.op("dve", lambda e, a=a, b=b, sh=sh: e.tensor_tensor(out=vmt[:, b, sh:32], in0=vmt[:, a, sh:32], in1=vmt[:, a, 0:32 - sh], op=ALU.add),
                 [bvm], [bvm])
            P.op("dve", lambda e, g=g, b=b: e.reciprocal(out=rc16[:, g, :], in_=vmt[:, b, 16:32]), [bvm], [brc])
            sh *= 2

        for g in range(4):
            w = 2 << g
            for cgi in range(2):
                colb = g * 1024 + cgi * 512
                banks = [0, 1, 2, 3]

                def halo_mm(k, dst, bdst, banks=banks):
                    a_t, ba = acc[4]
                    for j in range(4):
                        P.op("pe", lambda e, j=j, a_t=a_t, k=k: e.matmul(a_t[:, j * 16:(j + 1) * 16], lhsT=dst[:, j * 128:(j + 1) * 128],
                                                                      rhs=xnT[:, k, 0:HALO], start=(k == 0 and j == 0), stop=(k == KC - 1)),
                             [bdst, bxnT], [ba])

                stream_group(lambda k, colb=colb: A["wtok"][k * 128:(k + 1) * 128, colb:colb + 512], KC, True,
                             lambda k: (xnT[:, k, HALO:HALO + TP], [bxnT]), banks, halo_mm, wbuf=lambda k: [A["b_wtok"][2 * k + 1]])
                for j in range(4):
                    ch = cgi * 4 + j
                    u_t, bu = uext[j % 2]
                    P.op("act", lambda e, u_t=u_t, j=j: e.activation(out=u_t[:, 0:HALO], in_=acc[4][0][:, j * 16:(j + 1) * 16], func=AF.Copy),
                         [acc[4][1]], [bu])
                    P.op("act", lambda e, u_t=u_t, j=j: e.activation(out=u_t[:, HALO:HALO + TP], in_=acc[j][0][:, :], func=AF.Copy),
                         [acc[j][1]], [bu])
                    src, bsrc = u_t, bu
                    sh = 1
                    tmp = [sa, sbb]
                    for it in range(g + 1):
                        d_t, bd = tmp[it % 2]
                        P.op("pool", lambda e, src=src, d_t=d_t, sh=sh: e.tensor_tensor(
                            out=d_t[:, sh:528], in0=src[:, sh:528], in1=src[:, 0:528 - sh], op=ALU.add), [bsrc], [bd])
                        src, bsrc = d_t, bd
                        sh *= 2
                    P.op("dve", lambda e, src=src, u_t=u_t, ch=ch, w=w: e.scalar_tensor_tensor(
                        out=pooledT[:, ch, 16:TP], in0=src[:, 32:528], scalar=1.0 / w, in1=u_t[:, 32:528], op0=ALU.mult, op1=ALU.subtract),
                        [bsrc, bu], [bpool])
                    P.op("dve", lambda e, src=src, g=g: e.tensor_tensor(out=src[:, 16:32], in0=src[:, 16:32], in1=rc16[:, g, :], op=ALU.mult),
                         [bsrc, brc], [bsrc])
                    P.op("dve", lambda e, src=src, u_t=u_t, ch=ch: e.tensor_tensor(out=pooledT[:, ch, 0:16], in0=src[:, 16:32], in1=u_t[:, 16:32], op=ALU.subtract),
                         [bsrc, bu], [bpool])
            for cgi in range(2):
                banks = banks_next()
                stream_group(lambda k, g=g, cgi=cgi: A["pmix"][g * 1024 + k * 128:g * 1024 + (k + 1) * 128, cgi * 512:(cgi + 1) * 512], 8, False,
                             lambda k: (pooledT[:, k, :], [bpool]), banks, wbuf=lambda k, g=g: [A["b_pmix"][g]])
                for j in range(4):
                    ch = cgi * 4 + j
                    copy_op(P, evrr(), fwork[:, ch, :], acc[banks[j]][0][:, :], [acc[banks[j]][1]], [bfw[ch]])
            for cgi in range(2):
                banks = banks_next()
                colb = 4096 + g * 1024 + cgi * 512
                stream_group(lambda k, colb=colb: A["wtok"][k * 128:(k + 1) * 128, colb:colb + 512], KC, True,
                             lambda k: (xnT[:, k, HALO:HALO + TP], [bxnT]), banks, wbuf=lambda k: [A["b_wtok"][2 * k + 1]])
                for j in range(4):
                    ch = cgi * 4 + j
                    gch = g * 8 + ch
                    s_t, bs = sg[j % 2]
                    P.op("act", lambda e, s_t=s_t, b=banks[j]: e.activation(out=s_t[:, :], in_=acc[b][0][:, :], func=AF.Silu), [acc[banks[j]][1]], [bs])
                    P.op("dve", lambda e, s_t=s_t, ch=ch, gch=gch: e.scalar_tensor_tensor(
                        out=ypre[:, gch, :], in0=s_t[:, :], scalar=cv[:, CV_PSC + gch:CV_PSC + gch + 1], in1=fwork[:, ch, :],
                        op0=ALU.mult, op1=ALU.mult), [bs, bcv, bfw[ch]], [bypre])

        ogv = A["og_all"].rearrange("t (k p) f -> t p k f", p=128)
        if P.cnt.get("cc", 0) >= 17:
            P.raw("sp", lambda e: e.wait_ge(P.sem["cc"], 17), meta=("w", "cc", 17))
        else:
            P.dsem("cc")
        for q4 in range(4):
            P.dma("sp", "ogld", ogT[:, q4 * 8:(q4 + 1) * 8, :],
                  lambda e, q4=q4, p=p: ogv[bass.ds(pid(e) * 2 + p, 1), :, q4 * 8:(q4 + 1) * 8, :].rearrange("o p k f -> (o p) k f"),
                  writes=[bogT])

        if A.get("dbgog") is not None and p == 0:
            for q4 in range(4):
                P.dma("sp", "dbgog", A["dbgog"][:, q4 * 4096:(q4 + 1) * 4096], ogT[:, q4 * 8:(q4 + 1) * 8, :], reads=[bogT])
        for cg in range(8):
            banks = banks_next()
            stream_group(lambda k, cg=cg: A["wpp"][k * 128:(k + 1) * 128, cg * 512:(cg + 1) * 512], KC, False,
                         lambda k: (ypre[:, k, :], [bypre]), banks, wbuf=lambda k: [A["b_wpp"][k // 2]])
            for j in range(4):
                copy_op(P, evrr(), fwork[:, j, :], acc[banks[j]][0][:, :], [acc[banks[j]][1]], [bfw[j]])
            banks = banks_next()
            stream_group(lambda k, cg=cg: A["wtok"][k * 128:(k + 1) * 128, 8192 + cg * 512:8192 + (cg + 1) * 512], KC, True,
                         lambda k: (xnT[:, k, HALO:HALO + TP], [bxnT]), banks, wbuf=lambda k: [A["b_wtok"][2 * k + 1]])
            for j in range(4):
                s_t, bs = sg[j % 2]
                P.op("act", lambda e, s_t=s_t, b=banks[j]: e.activation(out=s_t[:, :], in_=acc[b][0][:, :], func=AF.Sigmoid), [acc[banks[j]][1]], [bs])
                P.op("dve", lambda e, s_t=s_t, j=j: e.tensor_tensor(out=fwork[:, j, :], in0=fwork[:, j, :], in1=s_t[:, :], op=ALU.mult),
                     [bs, bfw[j]], [bfw[j]])
            banks = banks_next()
            stream_group(lambda k, cg=cg: A["wdp"][k * 128:(k + 1) * 128, cg * 512:(cg + 1) * 512], KC, False,
                         lambda k: (ogT[:, k, :], [bogT]), banks, wbuf=lambda k: [A["b_wdp"][k // 2]])
            for j in range(4):
                copy_op(P, evrr(), fwork[:, 4 + j, :], acc[banks[j]][0][:, :], [acc[banks[j]][1]], [bfw[4 + j]])
            banks = banks_next()
            stream_group(lambda k, cg=cg: A["wtok"][k * 128:(k + 1) * 128, 12288 + cg * 512:12288 + (cg + 1) * 512], KC, True,
                         lambda k: (xnT[:, k, HALO:HALO + TP], [bxnT]), banks, wbuf=lambda k: [A["b_wtok"][2 * k + 1]])
            for j in range(4):
                s_t, bs = sg[j % 2]
                P.op("act", lambda e, s_t=s_t, b=banks[j]: e.activation(out=s_t[:, :], in_=acc[b][0][:, :], func=AF.Sigmoid), [acc[banks[j]][1]], [bs])
                P.op("dve", lambda e, s_t=s_t, j=j: e.tensor_tensor(out=fwork[:, 4 + j, :], in0=fwork[:, 4 + j, :], in1=s_t[:, :], op=ALU.mult),
                     [bs, bfw[4 + j]], [bfw[4 + j]])
                P.op("pool", lambda e, j=j, cg=cg: e.tensor_tensor(out=mrg[:, cg * 4 + j, :], in0=fwork[:, j, :], in1=fwork[:, 4 + j, :], op=ALU.add),
                     [bfw[j], bfw[4 + j]], [bmrg])

        P.barrier()
        ri = 0
        for eg in range(8):
            banks = banks_next()
            def load(k, eg=eg):
                i = (slab_i[0] + k) % NS
                P.dma("sp", f"wsf{i}", wsf[i][0][:, :], A["wo"][k * 128:(k + 1) * 128, eg * 512:(eg + 1) * 512], reads=[A["b_wo"][k // 2]], writes=[wsf[i][1]])

            def cast(k):
                i = (slab_i[0] + k) % NS
                copy_op(P, castrr(), wsb[i][0][:, :], wsf[i][0][:, :], [wsf[i][1]], [wsb[i][1]])

            def mm(k, banks=banks):
                i = (slab_i[0] + k) % NS
                for t4 in range(4):
                    a_t, ba = acc[banks[t4]]
                    P.op("pe", lambda e, a_t=a_t, t4=t4, i=i, k=k: e.matmul(a_t[:, :], lhsT=mrg[:, k, t4 * 128:(t4 + 1) * 128], rhs=wsb[i][0][:, :],
                                                                         start=(k == 0), stop=(k == KC - 1)), [wsb[i][1], bmrg], [ba])
            load(0)
            load(1)
            cast(0)
            for k in range(KC):
                if k + 2 < KC:
                    load(k + 2)
                if k + 1 < KC:
                    cast(k + 1)
                mm(k)
            slab_i[0] += KC
            for t4 in range(4):
                x_t, bx = xr[ri % 2]
                x_t = x_t[:, :]
                h_t, bh = hsb[ri % 2]
                ri += 1
                P.dma("sp", f"xr{ri % 2}", x_t,
                      A["xown"][base + HALO + t4 * 128:base + HALO + (t4 + 1) * 128, eg * 512:(eg + 1) * 512],
                      writes=[bx])
                P.op("dve", lambda e, h_t=h_t, x_t=x_t, b=banks[t4]: e.tensor_tensor(out=h_t, in0=acc[b][0][:, :], in1=x_t, op=ALU.add),
                     [acc[banks[t4]][1], bx], [bh])
                P.op("act", lambda e, h_t=h_t, t4=t4, eg=eg, s_t=sg[ri % 2][0]: e.activation(
                    out=s_t[:, :], in_=h_t, func=AF.Square, accum_out=ssqp[:, t4, eg:eg + 1]), [bh], [sg[ri % 2][1], bssqp])
                P.dma("sp", f"hst{ri % 2}", A["hscr"][base + t4 * 128:base + (t4 + 1) * 128, eg * 512:(eg + 1) * 512], h_t, reads=[bh])
        P.barrier()
        P.dma("sp", "fgl", fg32[:, 0:D], A["fgain"][0:1, :].partition_broadcast(128), writes=[bfg])
        P.op("dve", lambda e: e.tensor_reduce(out=rstd4[:, 0:4], in_=ssqp[:, :, :], axis=AX.X, op=ALU.add), [bssqp], [brstd4])
        P.op("act", lambda e: e.activation(out=rstd4[:, 0:4], in_=rstd4[:, 0:4], func=AF.Sqrt, scale=1.0 / D, bias=eps6[:, 0:1]),
             [brstd4, beps], [brstd4])
        P.op("dve", lambda e: e.reciprocal(out=rstd4[:, 4:8], in_=rstd4[:, 0:4]), [brstd4], [brstd4])
        for t4 in range(4):
            P.dma("sp", "hld", xs[:, :], A["hscr"][base + t4 * 128:base + (t4 + 1) * 128, :], writes=[bxs])
            P.op("dve", lambda e, t4=t4: e.scalar_tensor_tensor(out=xs[:, :], in0=xs[:, :], scalar=rstd4[:, 4 + t4:5 + t4], in1=fg32[:, 0:D],
                                                               op0=ALU.mult, op1=ALU.mult), [bxs, brstd4, bfg], [bxs])
            P.dma("sp", "ost", A["out"][base + t4 * 128:base + (t4 + 1) * 128, :], xs[:, :], reads=[bxs])
        P.barrier()
    es.close()


def build_nc():
    nc = bass.Bass("TRN2", target_bir_lowering=False)
    _pid_cache.clear()

    def din(name, shape, dt=F32):
        return nc.dram_tensor(name, shape, dt, kind="ExternalInput").ap()

    A = {}
    A["vmask"] = din("vmask", [1, HALO + TOK + 32])
    A["xown"] = din("xown", [HALO + TOK, D])
    A["wdn"] = din("wdn", [2, D, WDN_COLS])
    A["cvec"] = din("cvec", [128, CV_N])
    A["fgain"] = din("fgain", [1, D])
    A["cf32"] = din("cf32", [128, CF_N])
    A["identf"] = din("identf", [128, 128])
    A["identb"] = din("identb", [128, 128], BF16)
    A["out"] = nc.dram_tensor("out", [TOK, D], F32, kind="ExternalOutput").ap()
    A["hscr"] = nc.dram_tensor("hscr", [TOK, D], F32, kind="Internal").ap()
    P = Prog(nc)
    import os
    if os.environ.get("KCUT", ""):
        A["dbg"] = nc.dram_tensor("dbg", [128, 4096], F32, kind="ExternalOutput").ap()
        A["dbgb"] = nc.dram_tensor("dbgb", [128, 4096], BF16, kind="ExternalOutput").ap()
        A["xg"] = din("xg_dbg", [512, D])
        bx = Buf("xgd")
        A["b_xg"] = [bx, bx]
        dn_phase(nc, P, A, 1, 1)
        P.wait_all("sp")
        assert P.simulate(), "semaphore deadlock"
        P.emit()
        return nc
    A["og_send32"] = nc.dram_tensor("og_send2", [16, 512, 256], F32, kind="Internal").ap()
    A["og_all32"] = nc.dram_tensor("og_all2", [16, 4096, 256], F32, kind="Internal").ap()
    A["og_dummy"] = nc.dram_tensor("og_dummy", [4096, 256], F32, kind="Internal").ap()
    A["og_send"] = A["og_send32"].bitcast(BF16)
    A["og_all"] = A["og_all32"].bitcast(BF16)
    assert list(A["og_send"].shape) == [16, 512, 512] and list(A["og_all"].shape) == [16, 4096, 512], (A["og_send"].shape, A["og_all"].shape)
    A["xg"] = din("xfull", [S, D])
    A["b_xg"] = [Buf(f"xfull{i}") for i in range(S // 256)]
    shards = [("wtok", D // 8, 16384, 8), ("pmix", D // 8, 1024, 128),
              ("wpp", D // 8, D, 32), ("wdp", D // 8, D, 32), ("wo", D // 8, D, 32)]
    for name, rows, cols, rpc in shards:
        src = din(name + "_s", [rows, cols])
        stage = nc.dram_tensor("st_" + name, [rows, cols], F32, kind="Internal").ap()
        full = nc.dram_tensor("fu_" + name, [rows * 8, cols], F32, kind="Internal").ap()
        key = P.dsem("cc_" + name)
        bl = []
        nch = rows // rpc
        dummy = nc.dram_tensor("du_" + name, [rpc * 8, cols], F32, kind="Internal").ap()
        for i in range(nch):
            P.dma("sp", "g_" + name, stage[i * rpc:(i + 1) * rpc, :], src[i * rpc:(i + 1) * rpc, :])
        gk = "g_" + name
        P.raw("pool", lambda e, gk=gk, n=P.cnt[gk]: e.wait_ge(P.sem[gk], n), meta=("w", gk, P.cnt[gk]))
        for i in range(nch + 1):
            ii = min(i, nch - 1)
            dst = full[ii * 8 * rpc:(ii + 1) * 8 * rpc, :] if i < nch else dummy[:, :]
            P.cnt[key] += 1
            P.raw("pool", lambda e, stage=stage, dst=dst, key=key, ii=ii, rpc=rpc: e.collective_compute(
                "AllGather", ALU.bypass, replica_groups=[list(range(NCORE))], ins=[stage[ii * rpc:(ii + 1) * rpc, :]],
                outs=[dst]).then_inc(P.sem[key], 1), meta=("i", key, 1))
        for i in range(nch):
            bb = Buf(f"{name}{i}")
            bb.w = {key: min(i + 2, nch + 1)}
            bl.append(bb)
        A[name] = full
        A["b_" + name] = bl
        A["cr_" + name] = rpc * 8
    import os
    stage = int(os.environ.get("KSTAGE", "9"))
    if os.environ.get("KDBG", ""):
        A["dbgog"] = nc.dram_tensor("dbgog", [128, KC * TP], BF16, kind="ExternalOutput").ap()
    if stage == 1:
        dn_phase(nc, P, A, 1, 1)
    elif stage == 2:
        dn_phase(nc, P, A, 2, 2)
    elif stage >= 3:
        dn_phase(nc, P, A)
    if stage >= 4:
        token_phase(nc, P, A)
    P.wait_all("sp")
    assert P.simulate(), "semaphore deadlock"
    P.emit()
    return nc


def host_consts():
    cf = np.zeros((128, CF_N), np.float32)
    cf[:, CF_ONES:CF_ONES + 128] = 1.0
    eye = np.eye(64, dtype=np.float32)
    cf[0:64, CF_I2:CF_I2 + 128] = np.concatenate([eye, eye], axis=1)
    p = np.arange(64)[:, None]
    f = np.arange(64)[None, :]
    m1 = np.where(f > p, 0.0, -BIG)
    m2 = np.where(f < p, 0.0, -BIG)
    m3 = np.where(f >= p, 0.0, -BIG)
    m = np.stack([m1, m2, m3], axis=1).reshape(64, 192)
    cf[0:64, CF_MASK:CF_MASK + 384] = np.concatenate([m, m], axis=1)
    for h in range(2):
        o = CF_SEL + h * 64
        cf[h, o:o + 64] = 1.0
        cf[32 + h, o:o + 64] = 1.0
        o = CF_SEL + 128 + h * 64
        cf[h, o:o + 64] = -1.0
        o = CF_SEL + 256 + h * 128
        cf[h, o:o + 128] = 1.0
    cf[0:34, CF_ID34:CF_ID34 + 34] = np.eye(34, dtype=np.float32)
    return cf


def kernel(x, norm_gain, w_in, w_qkv_conv, pool_mix, pool_scale, w_pool_proj, a_log, dt_bias,
           dn_head_norm, w_dn_proj, w_out, final_norm_gain):
    f32 = np.float32
    x2 = np.asarray(x, f32).reshape(S, D)
    xpad = np.concatenate([np.zeros((HALO, D), f32), x2], axis=0)
    vmask = np.concatenate([np.zeros((1, HALO), f32), np.ones((1, S + 32), f32)], axis=1)
    w = np.asarray(w_in, f32)[0]
    conv = np.asarray(w_qkv_conv, f32)[0]
    cf = host_consts()
    identf = np.eye(128, dtype=f32)
    identb = np.eye(128).astype(ml_dtypes.bfloat16)
    pmix = np.asarray(pool_mix, f32)[0].reshape(4096, 1024)
    wpp = np.asarray(w_pool_proj, f32)[0]
    wdp = np.asarray(w_dn_proj, f32)[0]
    wo = np.asarray(w_out, f32)[0]
    fgain = np.asarray(final_norm_gain, f32).reshape(1, D)
    def shard(full, rpc, c):
        n, cols = full.shape
        return np.ascontiguousarray(full.reshape(n // (8 * rpc), 8, rpc, cols)[:, c].reshape(-1, cols))

    wtok = np.concatenate([w[:, 0:8192], w[:, 6 * 4096 + 64:]], axis=1)
    in_maps = []
    for c in range(NCORE):
        wdn = np.zeros((2, D, WDN_COLS), f32)
        cvec = np.zeros((128, CV_N), f32)
        cvec[:, CV_GAIN:CV_GAIN + 32] = np.asarray(norm_gain, f32)[0].reshape(32, 128).T
        cvec[:, CV_PSC:CV_PSC + 32] = np.asarray(pool_scale, f32)[0].reshape(32, 128).T
        cvec[:, CV_GN] = np.asarray(dn_head_norm, f32)[0]
        for hg in range(2):
            for hl in range(2):
                H = 4 * c + 2 * hg + hl
                for typ in range(3):
                    b = typ * 2 + hl
                    col = 8192 + typ * 4096 + H * 128
                    wdn[hg, :, b * 128:(b + 1) * 128] = w[:, col:col + 128]
                    cch = typ * 4096 + H * 128
                    cvec[:, CV_CONV + (hg * 6 + b) * 4:CV_CONV + (hg * 6 + b) * 4 + 4] = conv[:, cch:cch + 128].T
                colz = 8192 + 3 * 4096 + H * 128
                wdn[hg, :, (6 + hl) * 128:(7 + hl) * 128] = w[:, colz:colz + 128]
                wdn[hg, :, 1024 + hl] = w[:, 6 * 4096 + H]
                wdn[hg, :, 1056 + hl] = w[:, 6 * 4096 + 32 + H]
                cvec[hl, CV_ALOG + hg] = np.asarray(a_log, f32)[0, H]
                cvec[hl, CV_DTB + hg] = np.asarray(dt_bias, f32)[0, H]
        in_maps.append({"vmask": np.ascontiguousarray(vmask[:, c * TOK:c * TOK + HALO + TOK + 32]),
                        "xown": np.ascontiguousarray(xpad[c * TOK:c * TOK + HALO + TOK]),
                        "wdn": wdn, "xfull": x2, "wtok_s": shard(wtok, 8, c), "pmix_s": shard(pmix, 128, c),
                        "wpp_s": shard(wpp, 32, c), "wdp_s": shard(wdp, 32, c), "wo_s": shard(wo, 32, c), "cvec": cvec, "fgain": fgain, "cf32": cf,
                        "identf": identf, "identb": identb})
    nc = build_nc()
    res = run_bass_kernel_spmd(nc, in_maps, core_ids=list(range(NCORE)))
    import os
    if os.environ.get("KDBG", ""):
        np.save("dbgog.npy", np.asarray(res.results[0]["dbgog"]).astype(np.float32))
    out = np.concatenate([np.asarray(r["out"]) for r in res.results], axis=0)
    return out.reshape(1, S, D).astype(np.float32)
```
